# Optimizing a Trainium2 kernel written in Bass

```python
import jax, jax.numpy as jnp
from jax import lax
import numpy as np

D_MODEL = 1024
BATCH = 2
SEQ = 16384
DEPTH = 1

CHUNK = 64
SG_BLOCK = 128
A_WIDTH = 1024
A_GROUPS = 8
A_GROUP_DIM = A_WIDTH // A_GROUPS
B_WIDTH = 1024
B_GROUPS = 8
CONV_WIDTH = 3
D_FF = 4 * D_MODEL
N_BRANCHES = 2
EPS = 1e-6
IN_COLS = 2 * A_WIDTH + 3 * B_WIDTH + N_BRANCHES * D_MODEL

kernel_name = "hybrid_sgmlp_shortconv_gated_block"


def rmsnorm(x, g):
    xf = x.astype(jnp.float32)
    y = xf * lax.rsqrt(jnp.mean(xf * xf, axis=-1, keepdims=True) + EPS)
    return (y * g.astype(jnp.float32)).astype(x.dtype)


def chunk_mask():
    c = jnp.arange(SG_BLOCK) // CHUNK
    return c[None, :] <= c[:, None]


def spatial_gating(u, v, w_s, b_s):
    bsz, s, _ = v.shape
    vb = v.reshape(bsz, s // SG_BLOCK, SG_BLOCK, A_GROUPS, A_GROUP_DIM)
    w = jnp.where(chunk_mask()[None], w_s, jnp.zeros_like(w_s))
    mixed = jnp.einsum('gij,bnjgc->bnigc', w, vb) + b_s.T[None, None, :, :, None]
    return u * mixed.reshape(bsz, s, A_WIDTH)


def causal_dwconv(z, w):
    s = z.shape[1]
    zp = jnp.pad(z, ((0, 0), (CONV_WIDTH - 1, 0), (0, 0)))
    y = w[0] * zp[:, 0:s]
    for k in range(1, CONV_WIDTH):
        y = y + w[k] * zp[:, k:k + s]
    return y


def setup_inputs(seed: int = 0) -> dict:
    key = jax.random.key(seed)
    ks = jax.random.split(key, 16)
    f32 = jnp.float32
    nrm = lambda k, shape, scale: jax.random.normal(k, shape, f32) * scale
    return {
        "x": nrm(ks[0], (BATCH, SEQ, D_MODEL), 1.0),
        "norm_mix_g": 1.0 + nrm(ks[1], (DEPTH, D_MODEL), 0.1),
        "w_in": nrm(ks[2], (DEPTH, D_MODEL, IN_COLS), D_MODEL ** -0.5),
        "b_gate": nrm(ks[3], (DEPTH, N_BRANCHES * D_MODEL), 0.01),
        "norm_v_g": 1.0 + nrm(ks[4], (DEPTH, A_WIDTH), 0.1),
        "w_s": nrm(ks[5], (DEPTH, A_GROUPS, SG_BLOCK, SG_BLOCK), SG_BLOCK ** -0.5),
        "b_s": 1.0 + nrm(ks[6], (DEPTH, A_GROUPS, SG_BLOCK), 0.1),
        "conv_w": nrm(ks[7], (DEPTH, CONV_WIDTH, B_WIDTH), CONV_WIDTH ** -0.5),
        "w_proj_a": nrm(ks[8], (DEPTH, A_WIDTH, D_MODEL), A_WIDTH ** -0.5),
        "w_proj_b": nrm(ks[9], (DEPTH, B_WIDTH, D_MODEL), B_WIDTH ** -0.5),
        "w_out": nrm(ks[10], (DEPTH, D_MODEL, D_MODEL), D_MODEL ** -0.5),
        "norm_ff_g": 1.0 + nrm(ks[11], (DEPTH, D_MODEL), 0.1),
        "w_ff1": nrm(ks[12], (DEPTH, D_MODEL, D_FF), D_MODEL ** -0.5),
        "w_ff2": nrm(ks[13], (DEPTH, D_FF, D_MODEL), D_FF ** -0.5),
        "norm_final_g": 1.0 + nrm(ks[14], (D_MODEL,), 0.1),
    }


def reference(x, norm_mix_g, w_in, b_gate, norm_v_g, w_s, b_s, conv_w, w_proj_a, w_proj_b,
              w_out, norm_ff_g, w_ff1, w_ff2, norm_final_g):
    split_at = [A_WIDTH, 2 * A_WIDTH, 2 * A_WIDTH + B_WIDTH, 2 * A_WIDTH + 2 * B_WIDTH,
                2 * A_WIDTH + 3 * B_WIDTH, 2 * A_WIDTH + 3 * B_WIDTH + D_MODEL]
    for l in range(DEPTH):
        h = rmsnorm(x, norm_mix_g[l])
        proj = jnp.einsum('bsd,dc->bsc', h, w_in[l])
        u, v, bg, cg, xs, ga, gb = jnp.split(proj, split_at, axis=-1)
        ga = ga + b_gate[l, :D_MODEL]
        gb = gb + b_gate[l, D_MODEL:]

        u = jax.nn.gelu(u, approximate=False)
        v = rmsnorm(jax.nn.gelu(v, approximate=False), norm_v_g[l])
        a = spatial_gating(u, v, w_s[l], b_s[l])

        c = bg * causal_dwconv(cg * xs, conv_w[l])

        m = (jax.nn.sigmoid(ga) * jnp.einsum('bsc,cd->bsd', a, w_proj_a[l])
             + jax.nn.sigmoid(gb) * jnp.einsum('bsc,cd->bsd', c, w_proj_b[l]))
        x = x + jnp.einsum('bsd,de->bse', m, w_out[l])

        hf = rmsnorm(x, norm_ff_g[l])
        z = jax.nn.relu(jnp.einsum('bsd,df->bsf', hf, w_ff1[l]))
        x = x + jnp.einsum('bsf,fd->bsd', z * z, w_ff2[l])
    return rmsnorm(x, norm_final_g)
```

```python
import numpy as np
from contextlib import ExitStack

import concourse.bass as bass
import concourse.mybir as mybir
from concourse.bass_utils import run_bass_kernel_spmd

F32 = mybir.dt.float32
BF16 = mybir.dt.bfloat16
I32 = mybir.dt.int32
AF = mybir.ActivationFunctionType
ALU = mybir.AluOpType

NCORES = 8
D = 1024
NT = 512
NB = 4
TOK_PER_CORE = 4096
NTILES = TOK_PER_CORE // NT
EPS = 1e-6
NSL = 7
NSLAB = 36
MAGIC = float(0x5F3759DF)

TILE_STREAM = ([2, 3] + [6, 8, 4] + [7, 9, 5] + [0, 1] + [10, 14, 12, 16] + [11, 15, 13, 17]
               + [18, 19] + list(range(20, 28)) + list(range(28, 36)))
HALO_STREAM = [6, 8, 7, 9]


class _Sem:
    def __init__(self, h):
        self.h = h
        self.n = 0


class _Prog:
    def __init__(self, nc, es):
        self.nc = nc
        self.es = es
        self.q = {e: [] for e in ("pe", "act", "dve", "pool", "sp")}
        self.prog = {e: self.sem("prog_" + e) for e in ("pe", "act", "dve", "pool")}
        self.last_w = {}
        self.readers = {}

    def sem(self, name):
        return _Sem(self.es.enter_context(self.nc.semaphore(name)))

    def issue(self, eng, fn, reads=(), writes=(), dma_sem=None):
        waits = {}

        def addw(t):
            if t is None:
                return
            sem, val, teng = t
            if teng == "pe" and eng == "pe":
                return
            k = id(sem)
            if k not in waits or waits[k][1] < val:
                waits[k] = (sem, val)

        for b in reads:
            addw(self.last_w.get(b))
        for b in writes:
            addw(self.last_w.get(b))
            for t in self.readers.get(b, ()):
                addw(t)
        if dma_sem is not None:
            dma_sem.n += 16
            ticket = (dma_sem, dma_sem.n, "dma")
            inc = (dma_sem, 16)
        else:
            s = self.prog[eng]
            s.n += 1
            ticket = (s, s.n, eng)
            inc = (s, 1)
        for b in reads:
            self.readers.setdefault(b, []).append(ticket)
        for b in writes:
            self.last_w[b] = ticket
            self.readers[b] = []
        self.q[eng].append((fn, list(waits.values()), inc))
        return ticket

    def wait_only(self, eng, tickets):
        self.q[eng].append((None, [(t[0], t[1]) for t in tickets], None))

    def emit(self, block):
        def run(engine, items):
            waited = {}
            for fn, waits, inc in items:
                for sem, val in waits:
                    if waited.get(id(sem), 0) >= val:
                        continue
                    engine.wait_ge(sem.h, val)
                    waited[id(sem)] = val
                if fn is None:
                    continue
                ins = fn(engine)
                if inc is not None:
                    ins.then_inc(inc[0].h, inc[1])

        q = self.q

        @block.sync
        def _(e):
            run(e, q["sp"])

        @block.tensor
        def _(e):
            run(e, q["pe"])

        @block.scalar
        def _(e):
            run(e, q["act"])

        @block.vector
        def _(e):
            run(e, q["dve"])

        @block.gpsimd
        def _(e):
            run(e, q["pool"])


def build_nc(ntiles=NTILES, debug=False):
    nc = bass.Bass("TRN2", target_bir_lowering=False)
    dbg_sem = []
    ntok = ntiles * NT
    dt_in = lambda name, shape: nc.dram_tensor(name, shape, F32, kind="ExternalInput").ap()
    x = dt_in("x", [ntok, D])
    x_halo = dt_in("x_halo", [2, D])
    g_mix = dt_in("norm_mix_g", [D])
    w_in_a = dt_in("w_in_a", [D, 7 * 512])
    w_in_b = dt_in("w_in_b", [D, 7 * 512])
    b_gate = dt_in("b_gate", [2 * D])
    g_v = dt_in("norm_v_g", [D])
    w_s = dt_in("w_s", [8, 128, 128])
    b_s = dt_in("b_s", [8 * 128])
    conv_w = dt_in("conv_w", [3, D])
    w_pa = dt_in("w_proj_a", [D, D])
    w_pb = dt_in("w_proj_b", [D, D])
    w_o = dt_in("w_out", [D, D])
    g_ff = dt_in("norm_ff_g", [D])
    w_f1 = dt_in("w_ff1", [D, 4 * D])
    w_f2 = dt_in("w_ff2", [4 * D, D])
    g_fin = dt_in("norm_final_g", [D])
    y = nc.dram_tensor("y", [ntok, D], F32, kind="ExternalOutput").ap()
    wsc = nc.dram_tensor("wsc", [NSLAB, 128, 4096], BF16, kind="Internal").ap()

    def slab_src(s):
        if s < 7:
            src = w_in_a[:, s * 512:(s + 1) * 512]
        elif s < 14:
            src = w_in_b[:, (s - 7) * 512:(s - 6) * 512]
        elif s < 16:
            src = w_pa[:, (s - 14) * 512:(s - 13) * 512]
        elif s < 18:
            src = w_pb[:, (s - 16) * 512:(s - 15) * 512]
        elif s < 20:
            src = w_o[:, (s - 18) * 512:(s - 17) * 512]
        elif s < 28:
            src = w_f1[:, (s - 20) * 512:(s - 19) * 512]
        else:
            half, kg = divmod(s - 28, 4)
            src = w_f2[kg * 1024:(kg + 1) * 1024, half * 512:(half + 1) * 512]
        return src.rearrange("(kc p) c -> p kc c", p=128)

    with ExitStack() as es:
        P = _Prog(nc, es)
        sb = lambda name, shape, dt: es.enter_context(nc.sbuf_tensor(name, shape, dt))

        slab = [sb("slab%d" % i, [128, 8, 512], BF16) for i in range(NSL)]
        R = sb("R", [128, 16384], BF16)
        z2 = R[:, :].rearrange("p (f t) -> p f t", f=32)
        aT = R[:, 0:4096].rearrange("p (k t) -> p k t", k=8)
        cT = R[:, 4096:8192].rearrange("p (k t) -> p k t", k=8)
        mT = R[:, 8192:12288].rearrange("p (k t) -> p k t", k=8)
        vn = R[:, 12288:16384].rearrange("p (b c) -> p b c", b=4)
        xres = sb("xres", [128, NB, D], F32)
        stage = [sb("stage%d" % i, [128, D], F32) for i in range(2)]
        xb = [sb("xb%d" % i, [128, D], BF16) for i in range(2)]
        hT = sb("hT", [128, 8, NT], BF16)
        hfT = sb("hfT", [128, 8, NT], BF16)
        vg = [sb("vg%d" % i, [128, D], F32) for i in range(2)]
        ug = [sb("ug%d" % i, [128, NT], F32) for i in range(2)]
        cy = [sb("cy%d" % i, [128, NT], F32) for i in range(2)]
        zb = [sb("zb%d" % i, [128, NT + 2], F32) for i in range(2)]
        ta = [sb("ta%d" % i, [128, NT], F32) for i in range(2)]
        tb = [sb("tb%d" % i, [128, NT], F32) for i in range(2)]
        rl = [sb("rl%d" % i, [128, NT], F32) for i in range(2)]
        gm_b = sb("gm_b", [128, D], F32)
        gv_b = sb("gv_b", [128, D], F32)
        gf_b = sb("gf_b", [128, D], F32)
        gn_b = sb("gn_b", [128, D], F32)
        junk = sb("junk", [128, D], BF16)
        WsT = sb("WsT", [128, 8, 128], BF16)
        ident = sb("ident", [128, 128], BF16)
        identf = sb("identf", [128, 128], F32)
        ones_r = sb("ones_r", [1, 128], BF16)
        bs_f = sb("bs_f", [1, D], F32)
        bs_t = sb("bs_t", [1, D], F32)
        bs_hi = sb("bs_hi", [1, D], BF16)
        bs_lo = sb("bs_lo", [1, D], BF16)
        prow = sb("prow", [40, 128], F32)
        pcol = sb("pcol", [128, 40], F32)
        zhalo = sb("zhalo", [128, 8, 2], F32)
        hcg = sb("hcg", [128, 16], F32)
        NSM = 8
        small = sb("small", [128, NSM * 4], F32)

        psum = [es.enter_context(nc.psum_tensor("ps%d" % i, [128, 512], F32)) for i in range(8)]
        psT = [p[:].bitcast(BF16).rearrange("p (k t) -> p k t", k=8) for p in psum]

        cv = [P.sem("cv%d" % s) for s in range(NSLAB)]
        ld_slab = [P.sem("lds%d" % i) for i in range(NSL)]
        ld_stage = [P.sem("ldst%d" % i) for i in range(2)]
        ld_xr = P.sem("ldxr")
        st_y = P.sem("sty")
        ld_c = P.sem("ldc")

        st = {"bank": 0, "sm": 0, "xb": 0}

        def next_bank():
            b = st["bank"] % 8
            st["bank"] += 1
            return b

        def dbg(name, ap, reads):
            if not debug:
                return
            if not dbg_sem:
                dbg_sem.append(P.sem("dbg"))
            shape = list(ap.shape)
            dt_ = ap.dtype
            o = nc.dram_tensor("dbg_" + name, shape, dt_, kind="ExternalOutput").ap()
            P.issue("sp", lambda e: e.dma_start(out=o, in_=ap), reads, [], dma_sem=dbg_sem[0])

        def act(fn, reads, writes):
            return P.issue("act", fn, reads, writes)

        def dve(fn, reads, writes):
            return P.issue("dve", fn, reads, writes)

        def pool(fn, reads, writes):
            return P.issue("pool", fn, reads, writes)

        def pe(fn, reads, writes):
            return P.issue("pe", fn, reads, writes)

        def mm_group(out_ap, pairs):
            def fn(t):
                n = len(pairs)
                ins = None
                for i, (l, r) in enumerate(pairs):
                    ins = t.matmul(out_ap, lhsT=l, rhs=r, start=(i == 0), stop=(i == n - 1))
                return ins
            return fn

        def rstd_chain(ss_buf_reads):
            k = st["sm"] % NSM
            st["sm"] += 1
            c = k * 4
            SS = small[:, c:c + 1]
            A = small[:, c + 1:c + 2]
            T = small[:, c + 2:c + 3]
            Y = small[:, c + 3:c + 4]
            nSS, nA, nT, nY = ("sm", k, 0), ("sm", k, 1), ("sm", k, 2), ("sm", k, 3)

            def run():
                dve(lambda v: v.tensor_scalar(out=A, in0=SS, scalar1=1.0 / D, scalar2=EPS, op0=ALU.mult, op1=ALU.add),
                    [nSS], [nA])
                dve(lambda v: v.tensor_scalar(out=T.bitcast(I32), in0=A.bitcast(I32), scalar1=1, scalar2=None,
                                              op0=ALU.logical_shift_right), [nA], [nT])
                dve(lambda v: v.tensor_scalar(out=Y.bitcast(I32), in0=T.bitcast(I32), scalar1=-1.0, scalar2=MAGIC,
                                              op0=ALU.mult, op1=ALU.add), [nT], [nY])
                for _ in range(3):
                    dve(lambda v: v.scalar_tensor_tensor(out=T, in0=Y, scalar=A, in1=Y, op0=ALU.mult, op1=ALU.mult),
                        [nY, nA], [nT])
                    dve(lambda v: v.tensor_scalar(out=T, in0=T, scalar1=-0.5, scalar2=1.5, op0=ALU.mult, op1=ALU.add),
                        [nT], [nT])
                    dve(lambda v: v.tensor_tensor(out=Y, in0=Y, in1=T, op=ALU.mult), [nY, nT], [nY])
            return SS, Y, nSS, nY, run

        def sumsq(src_ap, src_names):
            SS, Y, nSS, nY, run = rstd_chain(None)
            act(lambda a: a.activation(out=junk[:], in_=src_ap, func=AF.Square, accum_out=SS), src_names, [nSS])
            run()
            return Y, nY

        stream = list(HALO_STREAM)
        for _ in range(ntiles):
            stream += TILE_STREAM
        ws = {"next_load": 0, "slot_free": [True] * NSL, "cur": 0}

        def try_loads():
            while ws["next_load"] < len(stream):
                seq = ws["next_load"]
                slot = seq % NSL
                if not ws["slot_free"][slot]:
                    break
                sid = stream[seq]
                ws["slot_free"][slot] = False
                ws["next_load"] += 1
                P.issue("sp", (lambda sid, slot: lambda e: e.dma_start(
                    out=slab[slot][:], in_=wsc[sid].rearrange("p (k c) -> p k c", k=8)))(sid, slot),
                    reads=[("wsc", sid)], writes=[("slab", slot)], dma_sem=ld_slab[slot])

        class SlabUse:
            pass

        def acquire(ids):
            out = []
            for sid in ids:
                seq = ws["cur"]
                assert stream[seq] == sid, (seq, stream[seq], sid)
                assert seq < ws["next_load"], "slab not yet scheduled for load"
                ws["cur"] += 1
                out.append((slab[seq % NSL], ("slab", seq % NSL), seq % NSL))
            return out

        def release(slots):
            for s in slots:
                ws["slot_free"][s] = True
            try_loads()

        pool(lambda g: g.memset(identf[:], 0.0), [], ["identf"])
        pool(lambda g: g.affine_select(out=identf[:], in_=identf[:], pattern=[[-1, 128]], compare_op=ALU.not_equal,
                                       fill=1.0, base=0, channel_multiplier=1), ["identf"], ["identf"])
        pool(lambda g: g.tensor_copy(out=ident[:], in_=identf[:]), ["identf"], ["ident"])
        pool(lambda g: g.memset(ones_r[:], 1.0), [], ["ones_r"])
        pool(lambda g: g.memset(stage[0][:], 0.0), [], [("stage", 0)])

        def cload(out_ap, in_ap, name):
            P.issue("sp", lambda e: e.dma_start(out=out_ap, in_=in_ap), [], [name],
                    dma_sem=P.sem("ldc%d" % len(P.q["sp"])))

        cload(gm_b[:], g_mix.partition_broadcast(128), "gm_b")
        cload(gv_b[:], g_v.partition_broadcast(128), "gv_b")
        cload(gf_b[:], g_ff.partition_broadcast(128), "gf_b")
        cload(gn_b[:], g_fin.partition_broadcast(128), "gn_b")
        cload(bs_f[:], b_s.rearrange("(o n) -> o n", o=1), "bs_f")
        cload(prow[0:16, :], b_gate.rearrange("(j p) -> j p", p=128), "prow_a")
        cload(prow[16:40, :], conv_w.rearrange("k (j p) -> (k j) p", p=128), "prow_b")
        cload(stage[1][:].rearrange("p (g j) -> p g j", g=8), w_s.rearrange("g i j -> i g j"), ("stage", 1))
        P.issue("sp", lambda e: e.dma_start(out=stage[0][0:2, :], in_=x_halo), [], [("stage", 0)], dma_sem=ld_stage[0])

        order = []
        for s in HALO_STREAM + TILE_STREAM:
            if s not in order:
                order.append(s)
        for s in order:
            P.issue("pool", (lambda s: lambda g: g.dma_start(
                out=wsc[s].rearrange("p (k c) -> p k c", k=8), in_=slab_src(s)))(s),
                [], [("wsc", s)], dma_sem=cv[s])

        dve(lambda v: v.tensor_copy(out=bs_hi[:], in_=bs_f[:]), ["bs_f"], ["bs_hi"])
        dve(lambda v: v.tensor_copy(out=bs_t[:], in_=bs_hi[:]), ["bs_hi"], ["bs_t"])
        dve(lambda v: v.tensor_tensor(out=bs_t[:], in0=bs_f[:], in1=bs_t[:], op=ALU.subtract), ["bs_f", "bs_t"], ["bs_t"])
        dve(lambda v: v.tensor_copy(out=bs_lo[:], in_=bs_t[:]), ["bs_t"], ["bs_lo"])

        dve(lambda v: v.tensor_copy(out=xb[1][:], in_=stage[1][:]), [("stage", 1)], [("xb", 1)])
        bk = next_bank()

        def ws_tr(t, bk=bk):
            ins = None
            for g in range(8):
                ins = t.transpose(out=psT[bk][:, g, :], in_=xb[1][:, g * 128:(g + 1) * 128], identity=ident[:])
            return ins
        pe(ws_tr, [("xb", 1), "ident"], [("P", bk)])
        act((lambda bk: lambda a: a.activation(out=WsT[:], in_=psT[bk], func=AF.Copy))(bk), [("P", bk)], ["WsT"])
        pool(lambda g: g.memset(WsT[64:128, :, 0:64], 0.0), ["WsT"], ["WsT"])

        bk = next_bank()
        pe((lambda bk: lambda t: t.matmul(psum[bk][:, 0:40], lhsT=prow[:, :], rhs=identf[0:40, 0:40], start=True, stop=True))(bk),
           ["prow_a", "prow_b", "identf"], [("P", bk)])
        act((lambda bk: lambda a: a.activation(out=pcol[:, 0:16], in_=psum[bk][:, 0:16], func=AF.Copy, scale=0.5))(bk),
            [("P", bk)], ["pcol_a"])
        act((lambda bk: lambda a: a.activation(out=pcol[:, 16:40], in_=psum[bk][:, 16:40], func=AF.Copy))(bk),
            [("P", bk)], ["pcol_b"])

        try_loads()

        def prenorm_p1(src_ap, src_names, gb_tile, gb_name):
            Y, nY = sumsq(src_ap, src_names)
            i = st["xb"] % 2
            st["xb"] += 1
            dve(lambda v: v.scalar_tensor_tensor(out=xb[i][:], in0=src_ap, scalar=Y, in1=gb_tile[:],
                                                 op0=ALU.mult, op1=ALU.mult),
                list(src_names) + [nY, gb_name], [("xb", i)])
            return i

        def prenorm_p2(i, dstT, dst_name, b):
            bk = next_bank()

            def tr(t):
                ins = None
                for kc in range(8):
                    ins = t.transpose(out=psT[bk][:, kc, :], in_=xb[i][:, kc * 128:(kc + 1) * 128], identity=ident[:])
                return ins
            pe(tr, [("xb", i), "ident"], [("P", bk)])
            act(lambda a: a.activation(out=dstT[:, :, b * 128:(b + 1) * 128], in_=psT[bk], func=AF.Copy),
                [("P", bk)], [dst_name])

        def prenorm_block(src_ap, src_names, gb_tile, gb_name, dstT, dst_name_fn, b):
            i = prenorm_p1(src_ap, src_names, gb_tile, gb_name)
            prenorm_p2(i, dstT, dst_name_fn(b), b)

        hT_names = [("hT", b) for b in range(NB)]
        hfT_names = [("hfT", b) for b in range(NB)]

        prenorm_block(stage[0][:], [("stage", 0)], gm_b, "gm_b", hT, lambda b: ("hT", b), 0)
        (s6, n6, k6), (s8, n8, k8), (s7, n7, k7), (s9, n9, k9) = acquire(HALO_STREAM)
        bkA = next_bank()
        bkB = next_bank()
        for j in range(8):
            sc, ncg = (s6, n6) if j < 4 else (s7, n7)
            sx, nxs = (s8, n8) if j < 4 else (s9, n9)
            j4 = j % 4
            pe(mm_group(psum[bkA][:, 2 * j:2 * j + 2],
                        [(sc[:, kc, j4 * 128:(j4 + 1) * 128], hT[:, kc, 0:2]) for kc in range(8)]),
               [ncg, ("hT", 0)], [("P", bkA)])
            pe(mm_group(psum[bkB][:, 2 * j:2 * j + 2],
                        [(sx[:, kc, j4 * 128:(j4 + 1) * 128], hT[:, kc, 0:2]) for kc in range(8)]),
               [nxs, ("hT", 0)], [("P", bkB)])
            if j == 3:
                release([k6, k8])
            if j == 7:
                release([k7, k9])
        act(lambda a: a.activation(out=hcg[:], in_=psum[bkA][:, 0:16], func=AF.Copy), [("P", bkA)], ["hcg"])
        dve(lambda v: v.tensor_tensor(out=zhalo[:].rearrange("p j t -> p (j t)"), in0=psum[bkB][:, 0:16], in1=hcg[:],
                                      op=ALU.mult),
            [("P", bkB), "hcg"], [("zhalo", j) for j in range(8)])

        pendA = {}

        def stageA_p1(t, b):
            i = (t * NB + b) % 2
            r0 = t * NT + b * 128
            P.issue("sp", lambda e: e.dma_start(out=stage[i][:], in_=x[r0:r0 + 128, :]),
                    [], [("stage", i)], dma_sem=ld_stage[i])
            pendA[(t, b)] = prenorm_p1(stage[i][:], [("stage", i)], gm_b, "gm_b")

        def stageA_p2(t, b):
            prenorm_p2(pendA.pop((t, b)), hT, ("hT", b), b)

        for b in range(NB):
            stageA_p1(0, b)
            stageA_p2(0, b)
        dbg("zhalo", zhalo[:].rearrange("p j t -> p (j t)"), [("zhalo", j) for j in range(8)])
        dbg("pcol", pcol[:], ["pcol_a", "pcol_b"])
        dbg("WsT", WsT[:].rearrange("p g i -> p (g i)"), ["WsT"])
        dbg("hT", hT[:].rearrange("p k t -> p (k t)"), hT_names)
        dbg("small0", small[:], [("sm", k, 3) for k in range(5)])

        for t in range(ntiles):
            P.issue("sp", (lambda t: lambda e: e.dma_start(
                out=xres[:], in_=x[t * NT:(t + 1) * NT, :].rearrange("(b p) d -> p b d", p=128)))(t),
                [], [("xres", b) for b in range(NB)], dma_sem=ld_xr)

            (sv0, nv0, kv0), (sv1, nv1, kv1) = acquire([2, 3])
            for b in range(NB):
                i = b % 2
                for half, (sv, nv) in enumerate(((sv0, nv0), (sv1, nv1))):
                    bk = next_bank()
                    pe(mm_group(psum[bk][:], [(hT[:, kc, b * 128:(b + 1) * 128], sv[:, kc, :]) for kc in range(8)]),
                       [nv, ("hT", b)], [("P", bk)])
                    act((lambda bk, i, half: lambda a: a.activation(
                        out=vg[i][:, half * 512:(half + 1) * 512], in_=psum[bk][:], func=AF.Gelu))(bk, i, half),
                        [("P", bk)], [("vg", i, half)])
                Y, nY = sumsq(vg[i][:], [("vg", i, 0), ("vg", i, 1)])
                dve((lambda i, b, Y: lambda v: v.scalar_tensor_tensor(
                    out=vn[:, b, :], in0=vg[i][:], scalar=Y, in1=gv_b[:], op0=ALU.mult, op1=ALU.mult))(i, b, Y),
                    [("vg", i, 0), ("vg", i, 1), nY, "gv_b"], [("vn", b)])
            release([kv0, kv1])
            if t == 0:
                dbg("vn", R[:, 12288:16384], [("vn", b) for b in range(NB)])

            for jj in range(2):
                (sCg, nCg, kCg), (sXs, nXs, kXs), (sBg, nBg, kBg) = acquire([6 + jj, 8 + jj, 4 + jj])
                for j4 in range(4):
                    j = jj * 4 + j4
                    i = j % 2
                    cs = slice(j4 * 128, (j4 + 1) * 128)
                    bk1 = next_bank()
                    pe(mm_group(psum[bk1][:], [(sCg[:, kc, cs], hT[:, kc, :]) for kc in range(8)]),
                       [nCg] + hT_names, [("P", bk1)])
                    act((lambda bk1, i: lambda a: a.activation(out=cy[i][:], in_=psum[bk1][:], func=AF.Copy))(bk1, i),
                        [("P", bk1)], [("cy", i)])
                    bk2 = next_bank()
                    pe(mm_group(psum[bk2][:], [(sXs[:, kc, cs], hT[:, kc, :]) for kc in range(8)]),
                       [nXs] + hT_names, [("P", bk2)])
                    dve((lambda bk2, i: lambda v: v.tensor_tensor(
                        out=zb[i][:, 2:NT + 2], in0=psum[bk2][:], in1=cy[i][:], op=ALU.mult))(bk2, i),
                        [("P", bk2), ("cy", i)], [("zb", i)])
                    pool((lambda i, j: lambda g: g.tensor_copy(out=zb[i][:, 0:2], in_=zhalo[:, j, :]))(i, j),
                         [("zhalo", j)], [("zbh", i)])
                    dve((lambda i, j: lambda v: v.tensor_scalar(
                        out=cy[i][:], in0=zb[i][:, 0:NT], scalar1=pcol[:, 16 + j:17 + j], scalar2=None, op0=ALU.mult))(i, j),
                        [("zb", i), ("zbh", i), "pcol_b"], [("cy", i)])
                    dve((lambda i, j: lambda v: v.scalar_tensor_tensor(
                        out=cy[i][:], in0=zb[i][:, 1:NT + 1], scalar=pcol[:, 24 + j:25 + j], in1=cy[i][:],
                        op0=ALU.mult, op1=ALU.add))(i, j),
                        [("zb", i), ("zbh", i), ("cy", i), "pcol_b"], [("cy", i)])
                    dve((lambda i, j: lambda v: v.scalar_tensor_tensor(
                        out=cy[i][:], in0=zb[i][:, 2:NT + 2], scalar=pcol[:, 32 + j:33 + j], in1=cy[i][:],
                        op0=ALU.mult, op1=ALU.add))(i, j),
                        [("zb", i), ("cy", i), "pcol_b"], [("cy", i)])
                    pool((lambda i, j: lambda g: g.tensor_copy(out=zhalo[:, j, :], in_=zb[i][:, NT:NT + 2]))(i, j),
                         [("zb", i)], [("zhalo", j)])
                    bk3 = next_bank()
                    pe(mm_group(psum[bk3][:], [(sBg[:, kc, cs], hT[:, kc, :]) for kc in range(8)]),
                       [nBg] + hT_names, [("P", bk3)])
                    dve((lambda bk3, i, j: lambda v: v.tensor_tensor(
                        out=cT[:, j, :], in0=psum[bk3][:], in1=cy[i][:], op=ALU.mult))(bk3, i, j),
                        [("P", bk3), ("cy", i)], [("cT", j)])
                release([kCg, kXs, kBg])
            if t == 0:
                dbg("cT", R[:, 4096:8192], [("cT", j) for j in range(8)])

            for gg in range(2):
                ((sU, nU, kU),) = acquire([gg])
                for g4 in range(4):
                    g = gg * 4 + g4
                    i = g % 2
                    cs = slice(g4 * 128, (g4 + 1) * 128)
                    bk1 = next_bank()
                    pe(mm_group(psum[bk1][:], [(sU[:, kc, cs], hT[:, kc, :]) for kc in range(8)]),
                       [nU] + hT_names, [("P", bk1)])
                    act((lambda bk1, i: lambda a: a.activation(out=ug[i][:], in_=psum[bk1][:], func=AF.Gelu))(bk1, i),
                        [("P", bk1)], [("ug", i)])
                    bk2 = next_bank()

                    def sp_fn(t_, bk2=bk2, g=g):
                        ins = None
                        for b in range(NB):
                            o = psum[bk2][:, b * 128:(b + 1) * 128]
                            t_.matmul(o, lhsT=vn[:, b, g * 128:(g + 1) * 128], rhs=WsT[:, g, :], start=True, stop=False)
                            t_.matmul(o, lhsT=ones_r[0:1, :], rhs=bs_hi[0:1, g * 128:(g + 1) * 128], start=False, stop=False)
                            ins = t_.matmul(o, lhsT=ones_r[0:1, :], rhs=bs_lo[0:1, g * 128:(g + 1) * 128], start=False, stop=True)
                        return ins
                    pe(sp_fn, [("vn", b) for b in range(NB)] + ["WsT", "ones_r", "bs_hi", "bs_lo"], [("P", bk2)])
                    dve((lambda bk2, i, g: lambda v: v.tensor_tensor(
                        out=aT[:, g, :], in0=psum[bk2][:], in1=ug[i][:], op=ALU.mult))(bk2, i, g),
                        [("P", bk2), ("ug", i)], [("aT", g)])
                release([kU])
            if t == 0:
                dbg("aT", R[:, 0:4096], [("aT", j) for j in range(8)])

            aT_names = [("aT", g) for g in range(8)]
            cT_names = [("cT", j) for j in range(8)]
            for dd in range(2):
                (sGa, nGa, kGa), (sPa, nPa, kPa), (sGb, nGb, kGb), (sPb, nPb, kPb) = acquire(
                    [10 + dd, 14 + dd, 12 + dd, 16 + dd])
                for d4 in range(4):
                    d = dd * 4 + d4
                    i = d % 2
                    cs = slice(d4 * 128, (d4 + 1) * 128)
                    bk = next_bank()
                    pe(mm_group(psum[bk][:], [(sGa[:, kc, cs], hT[:, kc, :]) for kc in range(8)]),
                       [nGa] + hT_names, [("P", bk)])
                    act((lambda bk, i, d: lambda a: a.activation(
                        out=ta[i][:], in_=psum[bk][:], func=AF.Tanh, scale=0.5, bias=pcol[:, d:d + 1]))(bk, i, d),
                        [("P", bk), "pcol_a"], [("ta", i)])
                    bk = next_bank()
                    pe(mm_group(psum[bk][:], [(sPa[:, kc, cs], aT[:, kc, :]) for kc in range(8)]),
                       [nPa] + aT_names, [("P", bk)])
                    dve((lambda bk, i: lambda v: v.scalar_tensor_tensor(
                        out=ta[i][:], in0=ta[i][:], scalar=1.0, in1=psum[bk][:], op0=ALU.add, op1=ALU.mult))(bk, i),
                        [("P", bk), ("ta", i)], [("ta", i)])
                    bk = next_bank()
                    pe(mm_group(psum[bk][:], [(sGb[:, kc, cs], hT[:, kc, :]) for kc in range(8)]),
                       [nGb] + hT_names, [("P", bk)])
                    act((lambda bk, i, d: lambda a: a.activation(
                        out=tb[i][:], in_=psum[bk][:], func=AF.Tanh, scale=0.5, bias=pcol[:, 8 + d:9 + d]))(bk, i, d),
                        [("P", bk), "pcol_a"], [("tb", i)])
                    bk = next_bank()
                    pe(mm_group(psum[bk][:], [(sPb[:, kc, cs], cT[:, kc, :]) for kc in range(8)]),
                       [nPb] + cT_names, [("P", bk)])
                    dve((lambda bk, i: lambda v: v.scalar_tensor_tensor(
                        out=tb[i][:], in0=tb[i][:], scalar=1.0, in1=psum[bk][:], op0=ALU.add, op1=ALU.mult))(bk, i),
                        [("P", bk), ("tb", i)], [("tb", i)])
                    pool((lambda i, d: lambda g: g.tensor_tensor(out=mT[:, d, :], in0=ta[i][:], in1=tb[i][:], op=ALU.add))(i, d),
                         [("ta", i), ("tb", i)], [("mT", d)])
                release([kGa, kPa, kGb, kPb])
            if t == 0:
                dbg("mT", R[:, 8192:12288], [("mT", j) for j in range(8)])

            mT_names = [("mT", d) for d in range(8)]
            (sO0, nO0, kO0), (sO1, nO1, kO1) = acquire([18, 19])
            pendF = None
            for b in range(NB):
                for half, (sO, nO) in enumerate(((sO0, nO0), (sO1, nO1))):
                    bk = next_bank()
                    pe(mm_group(psum[bk][:], [(mT[:, kc, b * 128:(b + 1) * 128], sO[:, kc, :]) for kc in range(8)]),
                       [nO] + mT_names, [("P", bk)])
                    dve((lambda bk, b, half: lambda v: v.scalar_tensor_tensor(
                        out=xres[:, b, half * 512:(half + 1) * 512], in0=psum[bk][:], scalar=0.5,
                        in1=xres[:, b, half * 512:(half + 1) * 512], op0=ALU.mult, op1=ALU.add))(bk, b, half),
                        [("P", bk), ("xres", b)], [("xres", b)])
                if pendF is not None:
                    prenorm_p2(pendF[0], hfT, ("hfT", pendF[1]), pendF[1])
                pendF = (prenorm_p1(xres[:, b, :], [("xres", b)], gf_b, "gf_b"), b)
            release([kO0, kO1])
            prenorm_p2(pendF[0], hfT, ("hfT", pendF[1]), pendF[1])
            if t == 0:
                dbg("x1", xres[:].rearrange("p b d -> p (b d)"), [("xres", b) for b in range(NB)])
                dbg("hfT", hfT[:].rearrange("p k t -> p (k t)"), hfT_names)

            for fg in range(8):
                if t + 1 < ntiles and 1 <= fg < 1 + NB:
                    stageA_p1(t + 1, fg - 1)
                ((sF, nF, kF),) = acquire([20 + fg])
                for f4 in range(4):
                    f = fg * 4 + f4
                    i = f % 2
                    cs = slice(f4 * 128, (f4 + 1) * 128)
                    bk = next_bank()
                    pe(mm_group(psum[bk][:], [(sF[:, kc, cs], hfT[:, kc, :]) for kc in range(8)]),
                       [nF] + hfT_names, [("P", bk)])
                    act((lambda bk, i: lambda a: a.activation(out=rl[i][:], in_=psum[bk][:], func=AF.Relu))(bk, i),
                        [("P", bk)], [("rl", i)])
                    pool((lambda i, f: lambda g: g.tensor_tensor(out=z2[:, f, :], in0=rl[i][:], in1=rl[i][:], op=ALU.mult))(i, f),
                         [("rl", i)], [("z2", f)])
                release([kF])
                if t + 1 < ntiles and 2 <= fg < 2 + NB:
                    stageA_p2(t + 1, fg - 2)

            for half in range(2):
                banks = [next_bank() for _ in range(NB)]
                for kg in range(4):
                    ((sW, nW, kW),) = acquire([28 + half * 4 + kg])
                    for b in range(NB):
                        def seg(t_, b=b, kg=kg, sW=sW, bk=banks[b]):
                            ins = None
                            for k8_ in range(8):
                                kc = kg * 8 + k8_
                                ins = t_.matmul(psum[bk][:], lhsT=z2[:, kc, b * 128:(b + 1) * 128], rhs=sW[:, k8_, :],
                                                start=(kg == 0 and k8_ == 0), stop=(kg == 3 and k8_ == 7))
                            return ins
                        pe(seg, [nW] + [("z2", kg * 8 + k) for k in range(8)], [("P", banks[b])])
                    release([kW])
                for b in range(NB):
                    dve((lambda bk, b, half: lambda v: v.tensor_tensor(
                        out=xres[:, b, half * 512:(half + 1) * 512], in0=psum[bk][:],
                        in1=xres[:, b, half * 512:(half + 1) * 512], op=ALU.add))(banks[b], b, half),
                        [("P", banks[b]), ("xres", b)], [("xres", b)])
                    if half == 1:
                        Y, nY = sumsq(xres[:, b, :], [("xres", b)])
                        dve((lambda b, Y: lambda v: v.scalar_tensor_tensor(
                            out=xres[:, b, :], in0=xres[:, b, :], scalar=Y, in1=gn_b[:], op0=ALU.mult, op1=ALU.mult))(b, Y),
                            [("xres", b), nY, "gn_b"], [("xres", b)])
                        r0 = t * NT + b * 128
                        P.issue("sp", (lambda b, r0: lambda e: e.dma_start(out=y[r0:r0 + 128, :], in_=xres[:, b, :]))(b, r0),
                                [("xres", b)], [], dma_sem=st_y)

        if dbg_sem:
            P.wait_only("sp", [(dbg_sem[0], dbg_sem[0].n, "dma")])
        P.wait_only("sp", [(st_y, st_y.n, "dma")])

        with nc.Block() as block:
            P.emit(block)
    return nc


_NC_CACHE = {}


def kernel(**inputs):
    x = np.ascontiguousarray(np.asarray(inputs["x"], dtype=np.float32))
    B, S, Dm = x.shape
    per_b = NCORES // B
    tok = S // per_b
    assert tok == TOK_PER_CORE and Dm == D

    def f(name):
        return np.ascontiguousarray(np.asarray(inputs[name], dtype=np.float32))

    shared = {
        "norm_mix_g": f("norm_mix_g").reshape(D),
        "w_in_a": np.ascontiguousarray(f("w_in").reshape(D, 7 * D)[:, :7 * 512]),
        "w_in_b": np.ascontiguousarray(f("w_in").reshape(D, 7 * D)[:, 7 * 512:]),
        "b_gate": f("b_gate").reshape(2 * D),
        "norm_v_g": f("norm_v_g").reshape(D),
        "w_s": f("w_s").reshape(8, 128, 128),
        "b_s": f("b_s").reshape(8 * 128),
        "conv_w": f("conv_w").reshape(3, D),
        "w_proj_a": f("w_proj_a").reshape(D, D),
        "w_proj_b": f("w_proj_b").reshape(D, D),
        "w_out": f("w_out").reshape(D, D),
        "norm_ff_g": f("norm_ff_g").reshape(D),
        "w_ff1": f("w_ff1").reshape(D, 4 * D),
        "w_ff2": f("w_ff2").reshape(4 * D, D),
        "norm_final_g": f("norm_final_g").reshape(D),
    }
    in_maps = []
    for c in range(NCORES):
        bi, ci = divmod(c, per_b)
        s0 = ci * tok
        halo = np.zeros((2, D), np.float32)
        if ci > 0:
            halo[:] = x[bi, s0 - 2:s0]
        m = dict(shared)
        m["x"] = np.ascontiguousarray(x[bi, s0:s0 + tok])
        m["x_halo"] = halo
        in_maps.append(m)
    if "nc" not in _NC_CACHE:
        _NC_CACHE["nc"] = build_nc()
    res = run_bass_kernel_spmd(_NC_CACHE["nc"], in_maps, core_ids=list(range(NCORES)))
    out = np.empty((B, S, D), np.float32)
    for c in range(NCORES):
        bi, ci = divmod(c, per_b)
        out[bi, ci * tok:(ci + 1) * tok] = res.results[c]["y"]
    return out
```
